# Optimizing a Trainium2 kernel written in Bass

```python
import jax, jax.numpy as jnp
from jax import lax
import numpy as np

D_MODEL = 2048
BATCH = 2
SEQ = 16384
DEPTH = 2

N_MIXERS = 2
CHUNK = 128
EPS = 1e-6
ROPE_BASE = 10000.0
RET_HEADS = 8
RET_QK_DIM = D_MODEL
RET_V_DIM = 2 * D_MODEL
RET_QK_HEAD = RET_QK_DIM // RET_HEADS
RET_V_HEAD = RET_V_DIM // RET_HEADS
RET_IN_DIM = 2 * RET_QK_DIM + 2 * RET_V_DIM
ML_INNER = 2 * D_MODEL
ML_HEADS = 4
ML_HEAD = ML_INNER // ML_HEADS
QKV_BLOCK = 4
N_QKV_BLOCKS = ML_INNER // QKV_BLOCK
CONV_WIDTH = 4
D_FF = -(-8 * D_MODEL // (3 * 256)) * 256
N_RET_LAYERS = (DEPTH + 1) // 2
N_ML_LAYERS = DEPTH // 2

kernel_name = 'retention_mlstm_interleaved_trunk'


def rms_norm(x, g):
    xf = x.astype(jnp.float32)
    y = xf * lax.rsqrt(jnp.mean(xf * xf, axis=-1, keepdims=True) + EPS)
    return (y * g.astype(jnp.float32)).astype(x.dtype)


def head_layer_norm(h, g):
    mu = jnp.mean(h, axis=-1, keepdims=True)
    var = jnp.mean(jnp.square(h - mu), axis=-1, keepdims=True)
    y = (h - mu) * lax.rsqrt(var + EPS)
    return y.reshape(h.shape[0], h.shape[1], -1) * g.astype(jnp.float32)


def to_chunks(t):
    b, s, h, d = t.shape
    return t.reshape(b, s // CHUNK, CHUNK, h, d).transpose(1, 0, 3, 2, 4)


def from_chunks(t):
    n, b, h, l, d = t.shape
    return t.transpose(1, 0, 3, 2, 4).reshape(b, n * l, h, d)


def rotary(t, cos, sin):
    half = t.shape[-1] // 2
    t1, t2 = t[..., :half], t[..., half:]
    return jnp.concatenate([t1 * cos - t2 * sin, t1 * sin + t2 * cos], axis=-1)


def retention_mixer(u, positions, w_in, gn_g, w_out):
    b, s, _ = u.shape
    f32 = jnp.float32
    q, k, v, g = jnp.split(u @ w_in, [RET_QK_DIM, 2 * RET_QK_DIM, 2 * RET_QK_DIM + RET_V_DIM], axis=-1)
    q = q.astype(f32).reshape(b, s, RET_HEADS, RET_QK_HEAD)
    k = k.astype(f32).reshape(b, s, RET_HEADS, RET_QK_HEAD)
    v = v.astype(f32).reshape(b, s, RET_HEADS, RET_V_HEAD)
    inv_freq = ROPE_BASE ** (-jnp.arange(RET_QK_HEAD // 2, dtype=f32) * (2.0 / RET_QK_HEAD))
    ang = positions.astype(f32)[..., None] * inv_freq
    cos = jnp.cos(ang)[:, :, None, :]
    sin = jnp.sin(ang)[:, :, None, :]
    q = rotary(q, cos, sin)
    k = rotary(k, cos, sin) * (RET_QK_HEAD ** -0.5)
    log_gamma = jnp.log(1.0 - 2.0 ** (-5.0 - jnp.arange(RET_HEADS, dtype=f32)))
    idx = jnp.arange(CHUNK, dtype=f32)
    rel = idx[:, None] - idx[None, :]
    d_intra = jnp.where(rel[None] >= 0, jnp.exp(jnp.maximum(rel, 0.0)[None] * log_gamma[:, None, None]), 0.0)
    xi = jnp.exp((idx + 1.0)[None] * log_gamma[:, None])
    zeta = jnp.exp((CHUNK - 1.0 - idx)[None] * log_gamma[:, None])
    chunk_decay = jnp.exp(CHUNK * log_gamma)

    def step(state, inp):
        qc, kc, vc = inp
        scores = jnp.einsum('bhid,bhjd->bhij', qc, kc) * d_intra
        out = (jnp.einsum('bhij,bhje->bhie', scores, vc)
               + jnp.einsum('bhid,bhde->bhie', qc * xi[None, :, :, None], state))
        state = (chunk_decay[None, :, None, None] * state
                 + jnp.einsum('bhjd,bhje->bhde', kc * zeta[None, :, :, None], vc))
        return state, out

    state0 = jnp.zeros((b, RET_HEADS, RET_QK_HEAD, RET_V_HEAD), f32)
    _, o = lax.scan(step, state0, (to_chunks(q), to_chunks(k), to_chunks(v)))
    y = head_layer_norm(from_chunks(o), gn_g)
    y = jax.nn.silu(g.astype(f32)) * y
    return y.astype(u.dtype) @ w_out


def mlstm_mixer(u, w_in, conv_w, conv_b, w_q, w_k, w_v, w_gate, b_gate, gn_g, skip, w_out):
    b, s, _ = u.shape
    f32 = jnp.float32
    xm, z = jnp.split(u @ w_in, 2, axis=-1)
    xp = jnp.pad(xm, ((0, 0), (CONV_WIDTH - 1, 0), (0, 0)))
    conv = conv_b + sum(xp[:, j:j + s, :] * conv_w[j] for j in range(CONV_WIDTH))
    xc = jax.nn.silu(conv)

    def block_diag(t, w):
        tb = t.reshape(b, s, N_QKV_BLOCKS, QKV_BLOCK)
        return jnp.einsum('bsni,noi->bsno', tb, w).reshape(b, s, ML_INNER)

    q = block_diag(xc, w_q)
    k = block_diag(xc, w_k)
    v = block_diag(xm, w_v)
    gates = (q @ w_gate[:ML_INNER] + k @ w_gate[ML_INNER:2 * ML_INNER]
             + v @ w_gate[2 * ML_INNER:] + b_gate).astype(f32)
    ig = gates[..., :ML_HEADS]
    log_f = jax.nn.log_sigmoid(gates[..., ML_HEADS:])

    def gate_chunks(t):
        return t.reshape(b, s // CHUNK, CHUNK, ML_HEADS).transpose(1, 0, 3, 2)

    ig_c = gate_chunks(ig)
    cum_f = jnp.cumsum(gate_chunks(log_f), axis=-1)
    qh = q.astype(f32).reshape(b, s, ML_HEADS, ML_HEAD)
    kh = k.astype(f32).reshape(b, s, ML_HEADS, ML_HEAD) * (ML_HEAD ** -0.5)
    vh = v.astype(f32).reshape(b, s, ML_HEADS, ML_HEAD)
    causal = jnp.tril(jnp.ones((CHUNK, CHUNK), dtype=bool))

    def step(carry, inp):
        c_mat, n_vec, m = carry
        qc, kc, vc, igc, bc = inp
        log_d = jnp.where(causal, bc[..., :, None] - bc[..., None, :] + igc[..., None, :], -jnp.inf)
        m_inter = bc + m[..., None]
        m_t = jnp.maximum(jnp.max(log_d, axis=-1), m_inter)
        scores = jnp.einsum('bhtd,bhsd->bhts', qc, kc) * jnp.exp(log_d - m_t[..., None])
        inter = jnp.exp(m_inter - m_t)
        num = (jnp.einsum('bhts,bhse->bhte', scores, vc)
               + inter[..., None] * jnp.einsum('bhtd,bhde->bhte', qc, c_mat))
        den = jnp.sum(scores, axis=-1) + inter * jnp.einsum('bhtd,bhd->bht', qc, n_vec)
        h = num / jnp.maximum(jnp.abs(den), jnp.exp(-m_t))[..., None]
        b_last = bc[..., -1]
        log_w = b_last[..., None] - bc + igc
        m_new = jnp.maximum(b_last + m, jnp.max(log_w, axis=-1))
        wts = jnp.exp(log_w - m_new[..., None])
        decay = jnp.exp(b_last + m - m_new)
        kw = kc * wts[..., None]
        c_mat = decay[..., None, None] * c_mat + jnp.einsum('bhsd,bhse->bhde', kw, vc)
        n_vec = decay[..., None] * n_vec + jnp.sum(kw, axis=-2)
        return (c_mat, n_vec, m_new), h

    carry0 = (jnp.zeros((b, ML_HEADS, ML_HEAD, ML_HEAD), f32),
              jnp.zeros((b, ML_HEADS, ML_HEAD), f32),
              jnp.zeros((b, ML_HEADS), f32))
    _, hc = lax.scan(step, carry0, (to_chunks(qh), to_chunks(kh), to_chunks(vh), ig_c, cum_f))
    hn = head_layer_norm(from_chunks(hc), gn_g)
    hs = (hn + skip.astype(f32) * xc.astype(f32)) * jax.nn.silu(z.astype(f32))
    return hs.astype(u.dtype) @ w_out


def swiglu(u, w_gate, w_up, w_down):
    return (jax.nn.silu(u @ w_gate) * (u @ w_up)) @ w_down


def setup_inputs(seed: int = 0) -> dict:
    key = jax.random.key(seed)
    ks = jax.random.split(key, 24)

    def nrm(k, shape, scale):
        return jax.random.normal(k, shape, jnp.float32) * scale

    x = jax.random.normal(ks[0], (BATCH, SEQ, D_MODEL), jnp.float32)
    positions = jnp.broadcast_to(jnp.arange(SEQ, dtype=jnp.int32), (BATCH, SEQ))
    norm_mix_g = 1.0 + nrm(ks[1], (DEPTH, D_MODEL), 0.02)
    norm_ffn_g = 1.0 + nrm(ks[2], (DEPTH, D_MODEL), 0.02)
    ret_w_in = nrm(ks[3], (N_RET_LAYERS, D_MODEL, RET_IN_DIM), D_MODEL ** -0.5)
    ret_gn_g = 1.0 + nrm(ks[4], (N_RET_LAYERS, RET_V_DIM), 0.02)
    ret_w_out = nrm(ks[5], (N_RET_LAYERS, RET_V_DIM, D_MODEL), RET_V_DIM ** -0.5)
    ml_w_in = nrm(ks[6], (N_ML_LAYERS, D_MODEL, 2 * ML_INNER), D_MODEL ** -0.5)
    ml_conv_w = nrm(ks[7], (N_ML_LAYERS, CONV_WIDTH, ML_INNER), CONV_WIDTH ** -0.5)
    ml_conv_b = nrm(ks[8], (N_ML_LAYERS, ML_INNER), 0.01)
    ml_w_q = nrm(ks[9], (N_ML_LAYERS, N_QKV_BLOCKS, QKV_BLOCK, QKV_BLOCK), QKV_BLOCK ** -0.5)
    ml_w_k = nrm(ks[10], (N_ML_LAYERS, N_QKV_BLOCKS, QKV_BLOCK, QKV_BLOCK), QKV_BLOCK ** -0.5)
    ml_w_v = nrm(ks[11], (N_ML_LAYERS, N_QKV_BLOCKS, QKV_BLOCK, QKV_BLOCK), QKV_BLOCK ** -0.5)
    ml_w_gate = nrm(ks[12], (N_ML_LAYERS, 3 * ML_INNER, 2 * ML_HEADS), (3 * ML_INNER) ** -0.5)
    ig_bias = nrm(ks[13], (N_ML_LAYERS, ML_HEADS), 0.1)
    fg_bias = jnp.linspace(3.0, 6.0, ML_HEADS, dtype=jnp.float32)[None] + nrm(ks[14], (N_ML_LAYERS, ML_HEADS), 0.01)
    ml_b_gate = jnp.concatenate([ig_bias, fg_bias], axis=-1)
    ml_gn_g = 1.0 + nrm(ks[15], (N_ML_LAYERS, ML_INNER), 0.02)
    ml_skip = 1.0 + nrm(ks[16], (N_ML_LAYERS, ML_INNER), 0.02)
    ml_w_out = nrm(ks[17], (N_ML_LAYERS, ML_INNER, D_MODEL), ML_INNER ** -0.5)
    ffn_w_gate = nrm(ks[18], (DEPTH, D_MODEL, D_FF), D_MODEL ** -0.5)
    ffn_w_up = nrm(ks[19], (DEPTH, D_MODEL, D_FF), D_MODEL ** -0.5)
    ffn_w_down = nrm(ks[20], (DEPTH, D_FF, D_MODEL), D_FF ** -0.5)
    final_g = 1.0 + nrm(ks[21], (D_MODEL,), 0.02)
    return {'x': x, 'positions': positions, 'norm_mix_g': norm_mix_g, 'norm_ffn_g': norm_ffn_g,
            'ret_w_in': ret_w_in, 'ret_gn_g': ret_gn_g, 'ret_w_out': ret_w_out,
            'ml_w_in': ml_w_in, 'ml_conv_w': ml_conv_w, 'ml_conv_b': ml_conv_b,
            'ml_w_q': ml_w_q, 'ml_w_k': ml_w_k, 'ml_w_v': ml_w_v,
            'ml_w_gate': ml_w_gate, 'ml_b_gate': ml_b_gate, 'ml_gn_g': ml_gn_g,
            'ml_skip': ml_skip, 'ml_w_out': ml_w_out,
            'ffn_w_gate': ffn_w_gate, 'ffn_w_up': ffn_w_up, 'ffn_w_down': ffn_w_down,
            'final_g': final_g}


def reference(x, positions, norm_mix_g, norm_ffn_g, ret_w_in, ret_gn_g, ret_w_out,
              ml_w_in, ml_conv_w, ml_conv_b, ml_w_q, ml_w_k, ml_w_v, ml_w_gate, ml_b_gate,
              ml_gn_g, ml_skip, ml_w_out, ffn_w_gate, ffn_w_up, ffn_w_down, final_g):
    h = x
    for i in range(DEPTH):
        j = i // N_MIXERS
        u = rms_norm(h, norm_mix_g[i])
        if i % N_MIXERS == 0:
            h = h + retention_mixer(u, positions, ret_w_in[j], ret_gn_g[j], ret_w_out[j])
        else:
            h = h + mlstm_mixer(u, ml_w_in[j], ml_conv_w[j], ml_conv_b[j], ml_w_q[j], ml_w_k[j],
                                ml_w_v[j], ml_w_gate[j], ml_b_gate[j], ml_gn_g[j], ml_skip[j],
                                ml_w_out[j])
        u = rms_norm(h, norm_ffn_g[i])
        h = h + swiglu(u, ffn_w_gate[i], ffn_w_up[i], ffn_w_down[i])
    return rms_norm(h, final_g)
```

```python
import numpy as np
import ml_dtypes
from contextlib import ExitStack
import concourse.bass as bass
import concourse.mybir as mybir
from concourse.bass_utils import run_bass_kernel_spmd

F32 = mybir.dt.float32
BF16 = mybir.dt.bfloat16
I32 = mybir.dt.int32
AF = mybir.ActivationFunctionType
ALU = mybir.AluOpType
AX = mybir.AxisListType
NPBF = ml_dtypes.bfloat16

D = 2048
NCORE = 8
EPS = 1e-6
DFF = 5632
SEM_LIMIT = 6000


class Buf:
    def __init__(self, t, dsem=None):
        self.t = t
        self.w = {}
        self.r = {}
        self.dsem = dsem
        self.dcnt = 0

    def __getitem__(self, k):
        return self.t[k]


class EngQ:
    def __init__(self, name, eng, inorder):
        self.name = name
        self.eng = eng
        self.inorder = inorder
        self.sem = None
        self.cnt = 0
        self.own = set()
        self.waited = {}


class Prog:
    def __init__(self):
        self.nc = bass.Bass("TRN2", target_bir_lowering=False)
        self.es = ExitStack()
        nc = self.nc
        self.pe = EngQ("pe", nc.tensor, True)
        self.dve = EngQ("dve", nc.vector, False)
        self.act = EngQ("act", nc.scalar, False)
        self.pool = EngQ("pool", nc.gpsimd, False)
        self.sp = EngQ("sp", nc.sync, False)
        self.nsem = 0
        self.nname = 0
        for e in (self.pe, self.dve, self.act, self.pool, self.sp):
            self._newsem(e)

    def sem(self, name):
        self.nsem += 1
        return self.es.enter_context(self.nc.semaphore("%s_%d" % (name, self.nsem)))

    def _newsem(self, e):
        e.sem = self.sem(e.name)
        e.cnt = 0
        e.own.add(id(e.sem))

    def name(self, n):
        self.nname += 1
        return "%s_%d" % (n, self.nname)

    def sb(self, name, shape, dtype, dma=False):
        t = self.es.enter_context(self.nc.sbuf_tensor(self.name(name), list(shape), dtype))
        return Buf(t, self.sem("d" + name) if dma else None)

    def ps(self, name, shape, dtype):
        t = self.es.enter_context(self.nc.psum_tensor(self.name(name), list(shape), dtype))
        return Buf(t)

    def dram(self, name, shape, dtype, kind):
        t = self.nc.dram_tensor(name, list(shape), dtype, kind=kind)
        return Buf(t.ap(), self.sem("D" + name))

    def _waits(self, E, reads, writes):
        need = {}
        for b in reads:
            for k, (s, v) in b.w.items():
                if k not in need or need[k][1] < v:
                    need[k] = (s, v)
        for b in writes:
            for dd in (b.w, b.r):
                for k, (s, v) in dd.items():
                    if k not in need or need[k][1] < v:
                        need[k] = (s, v)
        for k, (s, v) in need.items():
            if E.inorder and k in E.own:
                continue
            if E.waited.get(k, 0) >= v:
                continue
            E.eng.wait_ge(s, v)
            E.waited[k] = v

    def _done(self, tok, reads, writes, merge=False):
        k = id(tok[0])
        for b in writes:
            if merge:
                b.w[k] = tok
            else:
                b.w = {k: tok}
            b.r = {}
        for b in reads:
            if not any(b is w for w in writes):
                b.r[k] = tok

    def op(self, E, fn, reads, writes):
        self._waits(E, reads, writes)
        ins = fn(E.eng)
        if E.cnt >= SEM_LIMIT:
            self._newsem(E)
        E.cnt += 1
        ins.then_inc(E.sem, 1)
        self._done((E.sem, E.cnt), reads, writes)
        return ins

    def mm(self, out_ap, pairs, reads, writes, first=True, last=True):
        E = self.pe
        self._waits(E, reads, writes)
        n = len(pairs)
        for i, (l, r) in enumerate(pairs):
            ins = self.nc.tensor.matmul(out_ap, lhsT=l, rhs=r, start=(first and i == 0), stop=(last and i == n - 1))
        if E.cnt >= SEM_LIMIT:
            self._newsem(E)
        E.cnt += 1
        ins.then_inc(E.sem, 1)
        self._done((E.sem, E.cnt), reads, writes)

    def tr(self, out_ap, in_ap, ident_ap, reads, writes):
        return self.op(self.pe, lambda e: e.transpose(out=out_ap, in_=in_ap, identity=ident_ap), reads, writes)

    def dma(self, Q, out_ap, in_ap, reads, writes, semb, **kw):
        self._waits(Q, reads, writes)
        ins = Q.eng.dma_start(out=out_ap, in_=in_ap, **kw)
        semb.dcnt += 16
        ins.then_inc(semb.dsem, 16)
        self._done((semb.dsem, semb.dcnt), reads, writes, merge=True)

    def finish(self, outs):
        need = {}
        for b in outs:
            for k, (s, v) in b.w.items():
                need[k] = (s, v)
        for k, (s, v) in need.items():
            self.sp.eng.wait_ge(s, v)


class Ring:
    def __init__(self, bufs):
        self.bufs = bufs
        self.i = 0

    def next(self):
        b = self.bufs[self.i % len(self.bufs)]
        self.i += 1
        return b


def cast_w(P, dst, src, nelem):
    rows = nelem // 512
    step = 8192
    for r in range(0, rows, step):
        n = min(step, rows - r)
        P.dma(P.pool, dst.t[r:r + n, :], src.t[r:r + n, :], [src], [dst], dst)


def w_in(P, name, K, N):
    src = P.dram(name, [K * N // 512, 512], F32, "ExternalInput")
    dst = P.dram(name + "_bf", [K * N // 512, 512], BF16, "Internal")
    cast_w(P, dst, src, K * N)
    v = Buf(dst.t.rearrange("(k a) b -> k (a b)", k=K), dst.dsem)
    v.w = dst.w
    v.r = dst.r
    return v


def load_w(P, ring, wB, k0, nkc, col0, ncol):
    wt = ring.next()
    src = wB.t[k0 * 128:(k0 + nkc) * 128, col0:col0 + ncol].rearrange("(kc p) n -> p kc n", p=128)
    P.dma(P.sp, wt[:, 0:nkc, 0:ncol], src, [wB], [wt], wt)
    return wt


class Common:
    def __init__(self, P, nps_f32=6):
        self.P = P
        self.ident = P.sb("ident", [128, 128], BF16, dma=True)
        self.identd = P.dram("ident_bf", [128, 128], BF16, "ExternalInput")
        P.dma(P.sp, self.ident[:, :], self.identd.t, [self.identd], [self.ident], self.ident)
        self.junk = P.sb("junk", [128, D], BF16)
        self.ss = P.sb("ss", [128, 4], F32)
        self.ms = P.sb("ms", [128, 4], F32)
        self.sd = P.sb("sd", [128, 4], F32)
        self.rstd = P.sb("rstd", [128, 4], F32)
        self.ub = P.sb("ub", [128, 4, D], BF16)
        self.uT = P.sb("uT", [128, 16, 512], BF16)
        self.acc = Ring([P.ps("acc", [128, 512], F32) for _ in range(nps_f32)])
        self.ptr = Ring([P.ps("ptr", [128, 8, 128], BF16) for _ in range(2)])
        self.wring = Ring([P.sb("wt", [128, 16, 512], BF16, dma=True) for _ in range(3)])
        self.ev = 0

    def gain(self, name):
        P = self.P
        g = P.dram(name, [D], F32, "ExternalInput")
        gb = P.sb(name + "_bc", [128, D], F32, dma=True)
        P.dma(P.sp, gb[:, :], g.t.partition_broadcast(128), [g], [gb], gb)
        return gb

    def evac_eng(self):
        self.ev += 1
        return self.P.dve if self.ev % 2 else self.P.act

    def copy(self, E, out_ap, in_ap, reads, writes):
        if E is self.P.act:
            return self.P.op(E, lambda e: e.activation(out=out_ap, in_=in_ap, func=AF.Copy), reads, writes)
        return self.P.op(E, lambda e: e.tensor_copy(out=out_ap, in_=in_ap), reads, writes)

    def rstd_of(self, h):
        P = self.P
        for c in range(4):
            P.op(P.act, lambda e, c=c: e.activation(out=self.junk[:, :], in_=h[:, c, :], func=AF.Square,
                                                    accum_out=self.ss[:, c:c + 1]), [h], [self.junk, self.ss])
        P.op(P.dve, lambda e: e.tensor_scalar(out=self.ms[:, :], in0=self.ss[:, :], scalar1=1.0 / D, scalar2=EPS,
                                              op0=ALU.mult, op1=ALU.add), [self.ss], [self.ms])
        P.op(P.act, lambda e: e.activation(out=self.sd[:, :], in_=self.ms[:, :], func=AF.Sqrt), [self.ms], [self.sd])
        P.op(P.dve, lambda e: e.reciprocal(out=self.rstd[:, :], in_=self.sd[:, :]), [self.sd], [self.rstd])

    def norm_T(self, h, gb):
        P = self.P
        self.rstd_of(h)
        for c in range(4):
            P.op(P.dve, lambda e, c=c: e.scalar_tensor_tensor(out=self.ub[:, c, :], in0=h[:, c, :],
                                                             scalar=self.rstd[:, c:c + 1], in1=gb[:, :],
                                                             op0=ALU.mult, op1=ALU.mult),
                 [h, self.rstd, gb], [self.ub])
        for c in range(4):
            for half in range(2):
                pt = self.ptr.next()
                for j in range(8):
                    kc = half * 8 + j
                    P.tr(pt[:, j, :], self.ub[:, c, kc * 128:(kc + 1) * 128], self.ident[:, :],
                         [self.ub, self.ident], [pt])
                self.copy(self.evac_eng(), self.uT[:, half * 8:(half + 1) * 8, c * 128:(c + 1) * 128], pt[:, :, :],
                          [pt], [self.uT])
        return self.uT


def ident_np():
    return np.eye(128, dtype=np.float32).astype(NPBF)


def build_CF(TOK, final):
    P = Prog()
    C = Common(P)
    NT = TOK // 512
    hin = P.dram("hin", [TOK, D], F32, "ExternalInput")
    mixT = P.dram("mixT", [4096, TOK], BF16, "ExternalInput")
    g_ffn = C.gain("g_ffn")
    g_nxt = C.gain("g_nxt")
    wout = w_in(P, "w_out", 4096, D)
    wg = w_in(P, "w_gate", D, DFF)
    wu = w_in(P, "w_up", D, DFF)
    wd = w_in(P, "w_down", DFF, D)
    if final:
        out = P.dram("out", [TOK, D], F32, "ExternalOutput")
        outs = [out]
    else:
        wml = w_in(P, "ml_w_in", D, 8192)
        hout = P.dram("hout", [TOK, D], F32, "ExternalOutput")
        xmT = P.dram("xmT", [4096, TOK], F32, "ExternalOutput")
        zsT = P.dram("zsT", [4096, TOK], BF16, "ExternalOutput")
        outs = [hout, xmT, zsT]
        xst = P.sb("xst", [128, 4, 512], F32, dma=True)
        zst = P.sb("zst", [128, 4, 512], BF16, dma=True)
    h = P.sb("h", [128, 4, D], F32, dma=True)
    aT = P.sb("aT", [128, 44, 512], BF16, dma=True)
    stmp = Ring([P.sb("stmp", [128, 512], F32) for _ in range(2)])

    for t in range(NT):
        t0 = t * 512
        P.dma(P.sp, h[:, :, :], hin.t[t0:t0 + 512, :].rearrange("(c p) d -> p c d", p=128), [hin], [h], h)
        P.dma(P.sp, aT[:, 0:32, :], mixT.t[:, t0:t0 + 512].rearrange("(kc p) t -> p kc t", p=128), [mixT], [aT], aT)
        for nb in range(4):
            banks = [C.acc.next() for _ in range(4)]
            for piece in range(2):
                wt = load_w(P, C.wring, wout, piece * 16, 16, nb * 512, 512)
                for c in range(4):
                    P.mm(banks[c][:, :], [(aT[:, piece * 16 + j, c * 128:(c + 1) * 128], wt[:, j, :]) for j in range(16)],
                         [aT, wt], [banks[c]], first=(piece == 0), last=(piece == 1))
            for c in range(4):
                P.op(P.dve, lambda e, c=c: e.tensor_tensor(out=h[:, c, nb * 512:(nb + 1) * 512], in0=banks[c][:, :],
                                                          in1=h[:, c, nb * 512:(nb + 1) * 512], op=ALU.add),
                     [banks[c], h], [h])
        uT = C.norm_T(h, g_ffn)
        for fb in range(11):
            wgt = load_w(P, C.wring, wg, 0, 16, fb * 512, 512)
            wut = load_w(P, C.wring, wu, 0, 16, fb * 512, 512)
            for m in range(4):
                ba = C.acc.next()
                bb = C.acc.next()
                P.mm(ba[:, :], [(wgt[:, kc, m * 128:(m + 1) * 128], uT[:, kc, :]) for kc in range(16)], [wgt, uT], [ba])
                P.mm(bb[:, :], [(wut[:, kc, m * 128:(m + 1) * 128], uT[:, kc, :]) for kc in range(16)], [wut, uT], [bb])
                st = stmp.next()
                P.op(P.act, lambda e: e.activation(out=st[:, :], in_=ba[:, :], func=AF.Silu), [ba], [st])
                P.op(P.dve, lambda e: e.tensor_tensor(out=aT[:, fb * 4 + m, :], in0=bb[:, :], in1=st[:, :], op=ALU.mult),
                     [bb, st], [aT])
        for nb in range(4):
            banks = [C.acc.next() for _ in range(4)]
            for piece, (k0, nk) in enumerate(((0, 16), (16, 16), (32, 12))):
                wt = load_w(P, C.wring, wd, k0, nk, nb * 512, 512)
                for c in range(4):
                    P.mm(banks[c][:, :], [(aT[:, k0 + j, c * 128:(c + 1) * 128], wt[:, j, :]) for j in range(nk)],
                         [aT, wt], [banks[c]], first=(piece == 0), last=(piece == 2))
            for c in range(4):
                P.op(P.dve, lambda e, c=c: e.tensor_tensor(out=h[:, c, nb * 512:(nb + 1) * 512], in0=banks[c][:, :],
                                                          in1=h[:, c, nb * 512:(nb + 1) * 512], op=ALU.add),
                     [banks[c], h], [h])
        if final:
            C.rstd_of(h)
            for c in range(4):
                P.op(P.dve, lambda e, c=c: e.scalar_tensor_tensor(out=h[:, c, :], in0=h[:, c, :],
                                                                 scalar=C.rstd[:, c:c + 1], in1=g_nxt[:, :],
                                                                 op0=ALU.mult, op1=ALU.mult), [h, C.rstd, g_nxt], [h])
            P.dma(P.pool, out.t[t0:t0 + 512, :].rearrange("(c p) d -> p c d", p=128), h[:, :, :], [h], [out], h)
        else:
            P.dma(P.pool, hout.t[t0:t0 + 512, :].rearrange("(c p) d -> p c d", p=128), h[:, :, :], [h], [hout], h)
            uT = C.norm_T(h, g_nxt)
            for blk in range(16):
                wt = load_w(P, C.wring, wml, 0, 16, blk * 512, 512)
                for m in range(4):
                    ba = C.acc.next()
                    P.mm(ba[:, :], [(wt[:, kc, m * 128:(m + 1) * 128], uT[:, kc, :]) for kc in range(16)], [wt, uT], [ba])
                    if blk < 8:
                        C.copy(C.evac_eng(), xst[:, m, :], ba[:, :], [ba], [xst])
                    else:
                        P.op(P.act, lambda e, m=m: e.activation(out=zst[:, m, :], in_=ba[:, :], func=AF.Silu), [ba], [zst])
                if blk < 8:
                    P.dma(P.pool, xmT.t[blk * 512:(blk + 1) * 512, t0:t0 + 512].rearrange("(m p) t -> p m t", p=128),
                          xst[:, :, :], [xst], [xmT], xst)
                else:
                    b2 = blk - 8
                    P.dma(P.pool, zsT.t[b2 * 512:(b2 + 1) * 512, t0:t0 + 512].rearrange("(m p) t -> p m t", p=128),
                          zst[:, :, :], [zst], [zsT], zst)
    P.finish(outs)
    return P


def bcast_vec(P, name, n):
    g = P.dram(name, [n], F32, "ExternalInput")
    gb = P.sb(name + "_bc", [128, n], F32, dma=True)
    P.dma(P.sp, gb[:, :], g.t.partition_broadcast(128), [g], [gb], gb)
    return gb


def small_in(P, name, shape, dtype=F32):
    g = P.dram(name, list(shape), dtype, "ExternalInput")
    gb = P.sb(name + "_sb", list(shape), dtype, dma=True)
    P.dma(P.sp, gb.t[tuple(slice(None) for _ in shape)], g.t, [g], [gb], gb)
    return gb


TWO_PI = float(2 * np.pi)
C1 = 6.28125
C2 = float(np.float32(2 * np.pi - 6.28125))
PI = float(np.pi)


def build_A(TOK):
    P = Prog()
    C = Common(P)
    NT = TOK // 512
    x = P.dram("x", [TOK, D], F32, "ExternalInput")
    pos = P.dram("pos", [TOK], I32, "ExternalInput")
    g_mix = C.gain("g_mix")
    gng = bcast_vec(P, "gn_g", 4096)
    invf = small_in(P, "invf", [128, 1])
    zeta = small_in(P, "zeta", [128, 8])
    win = w_in(P, "ret_w_in", D, 12288)
    QT = P.dram("QT", [2048, TOK], BF16, "ExternalOutput")
    KT = P.dram("KT", [2048, TOK], BF16, "ExternalOutput")
    KZ = P.dram("KZ", [TOK, 2048], BF16, "ExternalOutput")
    V = P.dram("V", [TOK, 4096], BF16, "ExternalOutput")
    G = P.dram("G", [TOK, 4096], F32, "ExternalOutput")
    h = P.sb("h", [128, 4, D], F32, dma=True)
    posi = P.sb("posi", [128, 512], I32, dma=True)
    f = {n: P.sb(n, [128, 512], F32) for n in ("ang", "kf", "r", "m", "rc", "sin", "cos", "t1", "t2")}
    ki = P.sb("ki", [128, 512], I32)
    ro = Ring([P.sb("ro", [128, 2, 512], BF16, dma=True) for _ in range(2)])
    kzst = Ring([P.sb("kzst", [128, 4, 256], BF16, dma=True) for _ in range(2)])
    vst = Ring([P.sb("vst", [128, 4, 512], BF16, dma=True) for _ in range(2)])
    gst = Ring([P.sb("gst", [128, 4, 512], F32, dma=True) for _ in range(2)])
    stmp = Ring([P.sb("stmp", [128, 512], F32) for _ in range(2)])
    dv = P.dve

    def ts(out, in0, s1, s2, op0, op1, reads, writes):
        P.op(dv, lambda e: e.tensor_scalar(out=out, in0=in0, scalar1=s1, scalar2=s2, op0=op0, op1=op1), reads, writes)

    def stt(out, in0, sc, in1, op0, op1, reads, writes):
        P.op(dv, lambda e: e.scalar_tensor_tensor(out=out, in0=in0, scalar=sc, in1=in1, op0=op0, op1=op1), reads, writes)

    def tt(out, in0, in1, op, reads, writes):
        P.op(dv, lambda e: e.tensor_tensor(out=out, in0=in0, in1=in1, op=op), reads, writes)

    for t in range(NT):
        t0 = t * 512
        P.dma(P.sp, h[:, :, :], x.t[t0:t0 + 512, :].rearrange("(c p) d -> p c d", p=128), [x], [h], h)
        P.dma(P.sp, posi[:, :], pos.t[t0:t0 + 512].partition_broadcast(128), [pos], [posi], posi)
        a = lambda n: f[n][:, :]
        P.op(dv, lambda e: e.tensor_copy(out=a("kf"), in_=posi[:, :]), [posi], [f["kf"]])
        ts(a("ang"), a("kf"), invf[:, 0:1], None, ALU.mult, ALU.bypass, [f["kf"], invf], [f["ang"]])
        ts(ki[:, :], a("ang"), 1.0 / TWO_PI, None, ALU.mult, ALU.bypass, [f["ang"]], [ki])
        P.op(dv, lambda e: e.tensor_copy(out=a("kf"), in_=ki[:, :]), [ki], [f["kf"]])
        stt(a("r"), a("kf"), -C1, a("ang"), ALU.mult, ALU.add, [f["kf"], f["ang"]], [f["r"]])
        stt(a("r"), a("kf"), -C2, a("r"), ALU.mult, ALU.add, [f["kf"], f["r"]], [f["r"]])
        ts(a("m"), a("r"), PI, -TWO_PI, ALU.is_gt, ALU.mult, [f["r"]], [f["m"]])
        tt(a("r"), a("r"), a("m"), ALU.add, [f["r"], f["m"]], [f["r"]])
        ts(a("m"), a("r"), -PI, TWO_PI, ALU.is_lt, ALU.mult, [f["r"]], [f["m"]])
        tt(a("r"), a("r"), a("m"), ALU.add, [f["r"], f["m"]], [f["r"]])
        ts(a("rc"), a("r"), PI / 2, None, ALU.add, ALU.bypass, [f["r"]], [f["rc"]])
        ts(a("m"), a("rc"), PI, -TWO_PI, ALU.is_gt, ALU.mult, [f["rc"]], [f["m"]])
        tt(a("rc"), a("rc"), a("m"), ALU.add, [f["rc"], f["m"]], [f["rc"]])
        P.op(P.act, lambda e: e.activation(out=a("sin"), in_=a("r"), func=AF.Sin), [f["r"]], [f["sin"]])
        P.op(P.act, lambda e: e.activation(out=a("cos"), in_=a("rc"), func=AF.Sin), [f["rc"]], [f["cos"]])
        uT = C.norm_T(h, g_mix)
        for which in range(2):
            dst = QT if which == 0 else KT
            sc = 1.0 if which == 0 else 1.0 / 16.0
            for blk in range(4):
                wt = load_w(P, C.wring, win, 0, 16, which * 2048 + blk * 512, 512)
                for hh in range(2):
                    head = blk * 2 + hh
                    b1 = C.acc.next()
                    b2 = C.acc.next()
                    P.mm(b1[:, :], [(wt[:, kc, hh * 256:hh * 256 + 128], uT[:, kc, :]) for kc in range(16)], [wt, uT], [b1])
                    P.mm(b2[:, :], [(wt[:, kc, hh * 256 + 128:hh * 256 + 256], uT[:, kc, :]) for kc in range(16)], [wt, uT], [b2])
                    r_ = ro.next()
                    stt(a("t1"), b1[:, :], sc, a("cos"), ALU.mult, ALU.mult, [b1, f["cos"]], [f["t1"]])
                    stt(a("t2"), b2[:, :], sc, a("sin"), ALU.mult, ALU.mult, [b2, f["sin"]], [f["t2"]])
                    tt(r_[:, 0, :], a("t1"), a("t2"), ALU.subtract, [f["t1"], f["t2"]], [r_])
                    stt(a("t1"), b1[:, :], sc, a("sin"), ALU.mult, ALU.mult, [b1, f["sin"]], [f["t1"]])
                    stt(a("t2"), b2[:, :], sc, a("cos"), ALU.mult, ALU.mult, [b2, f["cos"]], [f["t2"]])
                    tt(r_[:, 1, :], a("t1"), a("t2"), ALU.add, [f["t1"], f["t2"]], [r_])
                    P.dma(P.pool, dst.t[head * 256:(head + 1) * 256, t0:t0 + 512].rearrange("(a p) t -> p a t", p=128),
                          r_[:, :, :], [r_], [dst], r_)
                    if which == 1:
                        kz_ = kzst.next()
                        pt = C.ptr.next()
                        for c in range(4):
                            for half in range(2):
                                P.tr(pt[:, c * 2 + half, :], r_[:, half, c * 128:(c + 1) * 128], C.ident[:, :],
                                     [r_, C.ident], [pt])
                        for c in range(4):
                            ts(kz_[:, c, :].rearrange("p (a d) -> p a d", a=2), pt[:, c * 2:c * 2 + 2, :],
                               zeta[:, head:head + 1], None, ALU.mult, ALU.bypass, [pt, zeta], [kz_])
                        P.dma(P.pool, KZ.t[t0:t0 + 512, head * 256:(head + 1) * 256].rearrange("(c p) d -> p c d", p=128),
                              kz_[:, :, :], [kz_], [KZ], kz_)
        for blk in range(16):
            wt = load_w(P, C.wring, win, 0, 16, 4096 + blk * 512, 512)
            isv = blk < 8
            st_ = vst.next() if isv else gst.next()
            for c in range(4):
                b = C.acc.next()
                P.mm(b[:, :], [(uT[:, kc, c * 128:(c + 1) * 128], wt[:, kc, :]) for kc in range(16)], [uT, wt], [b])
                if isv:
                    C.copy(C.evac_eng(), st_[:, c, :], b[:, :], [b], [st_])
                else:
                    s_ = stmp.next()
                    P.op(P.act, lambda e: e.activation(out=s_[:, :], in_=b[:, :], func=AF.Silu), [b], [s_])
                    tt(st_[:, c, :], s_[:, :], gng[:, (blk - 8) * 512:(blk - 7) * 512], ALU.mult, [s_, gng], [st_])
            if isv:
                P.dma(P.pool, V.t[t0:t0 + 512, blk * 512:(blk + 1) * 512].rearrange("(c p) n -> p c n", p=128),
                      st_[:, :, :], [st_], [V], st_)
            else:
                b2 = blk - 8
                P.dma(P.pool, G.t[t0:t0 + 512, b2 * 512:(b2 + 1) * 512].rearrange("(c p) n -> p c n", p=128),
                      st_[:, :, :], [st_], [G], st_)
    P.finish([QT, KT, KZ, V, G])
    return P


RET_GAMMA = [1.0 - 2.0 ** (-5.0 - hh) for hh in range(8)]


def ret_consts(h0):
    idx = np.arange(128, dtype=np.float64)
    dm = np.zeros((128, 2, 128), np.float32)
    xi = np.zeros((128, 2), np.float32)
    dec = np.zeros((128, 2), np.float32)
    for hl in range(2):
        lg = np.log(np.float32(RET_GAMMA[h0 + hl])).astype(np.float64)
        rel = idx[None, :] - idx[:, None]
        dm[:, hl, :] = np.where(rel >= 0, np.exp(np.maximum(rel, 0) * lg), 0.0)
        xi[:, hl] = np.exp((idx + 1) * lg)
        dec[:, hl] = np.exp(128 * lg)
    return dm, xi, dec


def zeta_np():
    idx = np.arange(128, dtype=np.float64)
    z = np.zeros((128, 8), np.float32)
    for hh in range(8):
        lg = np.log(np.float32(RET_GAMMA[hh])).astype(np.float64)
        z[:, hh] = np.exp((127 - idx) * lg)
    return z


def build_B(S):
    P = Prog()
    ident = small_in(P, "ident_bf", [128, 128], BF16)
    dm = small_in(P, "dm", [128, 2, 128])
    xi = small_in(P, "xi", [128, 2])
    dec = small_in(P, "dec", [128, 2])
    qT = P.dram("qT", [512, S], BF16, "ExternalInput")
    kT = P.dram("kT", [512, S], BF16, "ExternalInput")
    kz = P.dram("kz", [S, 512], BF16, "ExternalInput")
    v = P.dram("v", [S, 1024], BF16, "ExternalInput")
    g = P.dram("g", [S, 1024], F32, "ExternalInput")
    yT = P.dram("yT", [1024, S], BF16, "ExternalOutput")
    St = [P.sb("S", [128, 2, 512], F32) for _ in range(2)]
    Sb = [P.sb("Sb", [128, 2, 512], BF16) for _ in range(2)]
    for hl in range(2):
        P.op(P.pool, lambda e, hl=hl: e.memset(St[hl][:, :, :], 0.0), [], [St[hl]])
        P.op(P.pool, lambda e, hl=hl: e.memset(Sb[hl][:, :, :], 0.0), [], [Sb[hl]])
    qr = Ring([P.sb("qTt", [128, 2, 512], BF16, dma=True) for _ in range(2)])
    kr = Ring([P.sb("kTt", [128, 2, 512], BF16, dma=True) for _ in range(2)])
    kzr = Ring([P.sb("kzt", [128, 4, 256], BF16, dma=True) for _ in range(2)])
    vr = Ring([P.sb("vt", [128, 4, 512], BF16, dma=True) for _ in range(2)])
    gr = Ring([P.sb("gt", [128, 4, 512], F32, dma=True) for _ in range(2)])
    yr = Ring([P.sb("yTst", [128, 4, 512], BF16, dma=True) for _ in range(2)])
    bs = Ring([P.ps("bs", [128, 512], F32) for _ in range(1)])
    b1r = Ring([P.ps("b1", [128, 512], F32) for _ in range(2)])
    b2r = Ring([P.ps("b2", [128, 512], F32) for _ in range(1)])
    bur = Ring([P.ps("bu", [128, 512], F32) for _ in range(2)])
    ptr = Ring([P.ps("ptr", [128, 8, 128], BF16) for _ in range(2)])
    sTm = Ring([P.sb("sTm", [128, 128], BF16) for _ in range(2)])
    o2 = P.sb("o2", [128, 512], F32)
    o = P.sb("o", [128, 512], F32)
    st6 = P.sb("st6", [128, 6], F32)
    mv = P.sb("mv", [128, 2], F32)
    rs = P.sb("rs", [128, 1], F32)
    y1 = P.sb("y1", [128, 512], F32)
    yb = Ring([P.sb("yb", [128, 512], BF16) for _ in range(2)])
    dv = P.dve
    ev = [0]
    for grp in range(S // 512):
        t0 = grp * 512
        for hl in range(2):
            qt, kt, kzt, vt, gt, yst = qr.next(), kr.next(), kzr.next(), vr.next(), gr.next(), yr.next()
            P.dma(P.sp, qt[:, :, :], qT.t[hl * 256:(hl + 1) * 256, t0:t0 + 512].rearrange("(a p) t -> p a t", p=128), [qT], [qt], qt)
            P.dma(P.sp, kt[:, :, :], kT.t[hl * 256:(hl + 1) * 256, t0:t0 + 512].rearrange("(a p) t -> p a t", p=128), [kT], [kt], kt)
            P.dma(P.sp, kzt[:, :, :], kz.t[t0:t0 + 512, hl * 256:(hl + 1) * 256].rearrange("(c p) d -> p c d", p=128), [kz], [kzt], kzt)
            P.dma(P.sp, vt[:, :, :], v.t[t0:t0 + 512, hl * 512:(hl + 1) * 512].rearrange("(c p) d -> p c d", p=128), [v], [vt], vt)
            P.dma(P.sp, gt[:, :, :], g.t[t0:t0 + 512, hl * 512:(hl + 1) * 512].rearrange("(c p) d -> p c d", p=128), [g], [gt], gt)
            for c in range(4):
                cs = slice(c * 128, (c + 1) * 128)
                b_s = bs.next()
                P.mm(b_s[:, 0:128], [(kt[:, a_, cs], qt[:, a_, cs]) for a_ in range(2)], [kt, qt], [b_s])
                sm = sTm.next()
                P.op(dv, lambda e: e.tensor_tensor(out=sm[:, :], in0=b_s[:, 0:128], in1=dm[:, hl, :], op=ALU.mult), [b_s, dm], [sm])
                b1 = b1r.next()
                P.mm(b1[:, :], [(sm[:, :], vt[:, c, :])], [sm, vt], [b1])
                b2 = b2r.next()
                P.mm(b2[:, :], [(qt[:, a_, cs], Sb[hl][:, a_, :]) for a_ in range(2)], [qt, Sb[hl]], [b2])
                P.op(P.act, lambda e: e.activation(out=o2[:, :], in_=b2[:, :], func=AF.Copy, scale=xi[:, hl:hl + 1]), [b2, xi], [o2])
                P.op(dv, lambda e: e.tensor_tensor(out=o[:, :], in0=b1[:, :], in1=o2[:, :], op=ALU.add), [b1, o2], [o])
                for a_ in range(2):
                    bu = bur.next()
                    P.mm(bu[:, :], [(kzt[:, c, a_ * 128:(a_ + 1) * 128], vt[:, c, :])], [kzt, vt], [bu])
                    P.op(dv, lambda e, a_=a_, bu=bu: e.scalar_tensor_tensor(out=St[hl][:, a_, :], in0=St[hl][:, a_, :],
                                                                           scalar=dec[:, hl:hl + 1], in1=bu[:, :],
                                                                           op0=ALU.mult, op1=ALU.add),
                         [St[hl], dec, bu], [St[hl]])
                P.op(P.act, lambda e: e.activation(out=Sb[hl][:, :, :], in_=St[hl][:, :, :], func=AF.Copy), [St[hl]], [Sb[hl]])
                P.op(dv, lambda e: e.bn_stats(out=st6[:, :], in_=o[:, :]), [o], [st6])
                P.op(dv, lambda e: e.bn_aggr(out=mv[:, :], in_=st6[:, :]), [st6], [mv])
                P.op(dv, lambda e: e.tensor_scalar(out=rs[:, :], in0=mv[:, 1:2], scalar1=EPS, scalar2=None, op0=ALU.add, op1=ALU.bypass), [mv], [rs])
                P.op(P.act, lambda e: e.activation(out=rs[:, :], in_=rs[:, :], func=AF.Sqrt), [rs], [rs])
                P.op(dv, lambda e: e.reciprocal(out=rs[:, :], in_=rs[:, :]), [rs], [rs])
                P.op(dv, lambda e: e.tensor_scalar(out=y1[:, :], in0=o[:, :], scalar1=mv[:, 0:1], scalar2=rs[:, 0:1],
                                                  op0=ALU.subtract, op1=ALU.mult), [o, mv, rs], [y1])
                y_ = yb.next()
                P.op(dv, lambda e: e.tensor_tensor(out=y_[:, :], in0=y1[:, :], in1=gt[:, c, :], op=ALU.mult), [y1, gt], [y_])
                pt = ptr.next()
                for ec in range(4):
                    P.tr(pt[:, ec, :], y_[:, ec * 128:(ec + 1) * 128], ident[:, :], [y_, ident], [pt])
                ev[0] += 1
                if ev[0] % 2:
                    P.op(P.act, lambda e: e.activation(out=yst[:, :, cs], in_=pt[:, 0:4, :], func=AF.Copy), [pt], [yst])
                else:
                    P.op(dv, lambda e: e.tensor_copy(out=yst[:, :, cs], in_=pt[:, 0:4, :]), [pt], [yst])
            P.dma(P.pool, yT.t[hl * 512:(hl + 1) * 512, t0:t0 + 512].rearrange("(ec p) t -> p ec t", p=128), yst[:, :, :], [yst], [yT], yst)
    P.finish([yT])
    return P


def build_D(S):
    P = Prog()
    NT = S // 512
    xmT = P.dram("xmT", [1024, S], F32, "ExternalInput")
    cw = small_in(P, "conv_w", [128, 8, 4])
    cb = small_in(P, "conv_b", [128, 8])
    wf = {n: small_in(P, n, [128, 8, 128]) for n in ("wq_bd", "wk_bd", "wv_bd")}
    wgf = small_in(P, "wgate", [128, 3, 8, 8])
    wb = {n: P.sb(n + "b", [128, 8, 128], BF16) for n in wf}
    for n in wf:
        P.op(P.dve, lambda e, n=n: e.tensor_copy(out=wb[n][:, :, :], in_=wf[n][:, :, :]), [wf[n]], [wb[n]])
    wgb = P.sb("wgb", [128, 3, 8, 8], BF16)
    P.op(P.dve, lambda e: e.tensor_copy(out=wgb[:, :, :, :], in_=wgf[:, :, :, :]), [wgf], [wgb])
    P.op(P.dve, lambda e: e.tensor_scalar(out=wgb[:, 1, :, :], in0=wgf[:, 1, :, :], scalar1=32.0, scalar2=None,
                                          op0=ALU.mult, op1=ALU.bypass), [wgf, wgb], [wgb])
    qT = P.dram("qT", [1024, S], BF16, "ExternalOutput")
    kT = P.dram("kT", [1024, S], BF16, "ExternalOutput")
    ktok = P.dram("ktok", [S, 1024], BF16, "ExternalOutput")
    vtok = P.dram("vtok", [S, 1024], BF16, "ExternalOutput")
    xcT = P.dram("xcT", [1024, S], BF16, "ExternalOutput")
    gp = P.dram("gp", [S, 8], F32, "ExternalOutput")
    xm = Ring([P.sb("xm", [128, 8, 515], F32, dma=True) for _ in range(2)])
    xmb = P.sb("xmb", [128, 8, 512], BF16)
    acc = P.sb("cacc", [128, 512], F32)
    xcb = Ring([P.sb("xcb", [128, 8, 512], BF16, dma=True) for _ in range(2)])
    qTb = Ring([P.sb("qTb", [128, 8, 512], BF16, dma=True) for _ in range(2)])
    kTb = Ring([P.sb("kTb", [128, 8, 512], BF16, dma=True) for _ in range(2)])
    vTb = P.sb("vTb", [128, 8, 512], BF16)
    kst = Ring([P.sb("kst", [128, 4, 1024], BF16, dma=True) for _ in range(2)])
    vst = Ring([P.sb("vst", [128, 4, 1024], BF16, dma=True) for _ in range(2)])
    gst = Ring([P.sb("gst", [128, 4, 8], F32, dma=True) for _ in range(2)])
    bank = Ring([P.ps("bk", [128, 512], F32) for _ in range(7)])
    dv = P.dve
    ev = [0]

    def evac(out_ap, in_ap, scale, reads, writes):
        ev[0] += 1
        if ev[0] % 2:
            P.op(P.act, lambda e: e.activation(out=out_ap, in_=in_ap, func=AF.Copy, scale=scale), reads, writes)
        else:
            P.op(dv, lambda e: e.tensor_scalar(out=out_ap, in0=in_ap, scalar1=scale, scalar2=None, op0=ALU.mult,
                                               op1=ALU.bypass), reads, writes)

    for t in range(NT):
        t0 = t * 512
        x_ = xm.next()
        if t == 0:
            P.op(P.pool, lambda e: e.memset(x_[:, :, 0:3], 0.0), [], [x_])
            P.dma(P.sp, x_[:, :, 3:515], xmT.t[:, 0:512].rearrange("(kc p) t -> p kc t", p=128), [xmT], [x_], x_)
        else:
            P.dma(P.sp, x_[:, :, :], xmT.t[:, t0 - 3:t0 + 512].rearrange("(kc p) t -> p kc t", p=128), [xmT], [x_], x_)
        P.op(P.pool, lambda e: e.tensor_copy(out=xmb[:, :, :], in_=x_[:, :, 3:515]), [x_], [xmb])
        xc_ = xcb.next()
        for kc in range(8):
            P.op(dv, lambda e: e.tensor_scalar(out=acc[:, :], in0=x_[:, kc, 3:515], scalar1=cw[:, kc, 3:4], scalar2=cb[:, kc:kc + 1],
                                               op0=ALU.mult, op1=ALU.add), [x_, cw, cb], [acc])
            for j in (2, 1, 0):
                P.op(dv, lambda e, j=j: e.scalar_tensor_tensor(out=acc[:, :], in0=x_[:, kc, j:j + 512], scalar=cw[:, kc, j:j + 1],
                                                              in1=acc[:, :], op0=ALU.mult, op1=ALU.add), [x_, cw, acc], [acc])
            P.op(P.act, lambda e: e.activation(out=xc_[:, kc, :], in_=acc[:, :], func=AF.Silu), [acc], [xc_])
        P.dma(P.pool, xcT.t[:, t0:t0 + 512].rearrange("(kc p) t -> p kc t", p=128), xc_[:, :, :], [xc_], [xcT], xc_)
        q_, k_ = qTb.next(), kTb.next()
        for kc in range(8):
            b = bank.next()
            P.mm(b[:, :], [(wb["wq_bd"][:, kc, :], xc_[:, kc, :])], [wb["wq_bd"], xc_], [b])
            evac(q_[:, kc, :], b[:, :], 1.0, [b], [q_])
            b = bank.next()
            P.mm(b[:, :], [(wb["wk_bd"][:, kc, :], xc_[:, kc, :])], [wb["wk_bd"], xc_], [b])
            evac(k_[:, kc, :], b[:, :], 1.0 / 32.0, [b], [k_])
            b = bank.next()
            P.mm(b[:, :], [(wb["wv_bd"][:, kc, :], xmb[:, kc, :])], [wb["wv_bd"], xmb], [b])
            evac(vTb[:, kc, :], b[:, :], 1.0, [b], [vTb])
        P.dma(P.pool, qT.t[:, t0:t0 + 512].rearrange("(kc p) t -> p kc t", p=128), q_[:, :, :], [q_], [qT], q_)
        P.dma(P.pool, kT.t[:, t0:t0 + 512].rearrange("(kc p) t -> p kc t", p=128), k_[:, :, :], [k_], [kT], k_)
        ks_, vs_, gs_ = kst.next(), vst.next(), gst.next()
        for c in range(4):
            cs = slice(c * 128, (c + 1) * 128)
            for half in range(2):
                b = bank.next()
                for j in range(4):
                    kc = half * 4 + j
                    P.mm(b[:, j * 128:(j + 1) * 128], [(xc_[:, kc, cs], wb["wk_bd"][:, kc, :])], [xc_, wb["wk_bd"]], [b])
                evac(ks_[:, c, half * 512:(half + 1) * 512], b[:, :], 1.0 / 32.0, [b], [ks_])
                b = bank.next()
                for j in range(4):
                    kc = half * 4 + j
                    P.mm(b[:, j * 128:(j + 1) * 128], [(xmb[:, kc, cs], wb["wv_bd"][:, kc, :])], [xmb, wb["wv_bd"]], [b])
                evac(vs_[:, c, half * 512:(half + 1) * 512], b[:, :], 1.0, [b], [vs_])
            b = bank.next()
            pairs = []
            for wi, src in enumerate((q_, k_, vTb)):
                for kc in range(8):
                    pairs.append((src[:, kc, cs], wgb[:, wi, kc, :]))
            P.mm(b[:, 0:8], pairs, [q_, k_, vTb, wgb], [b])
            P.op(dv, lambda e: e.tensor_copy(out=gs_[:, c, :], in_=b[:, 0:8]), [b], [gs_])
        P.dma(P.pool, ktok.t[t0:t0 + 512, :].rearrange("(c p) d -> p c d", p=128), ks_[:, :, :], [ks_], [ktok], ks_)
        P.dma(P.pool, vtok.t[t0:t0 + 512, :].rearrange("(c p) d -> p c d", p=128), vs_[:, :, :], [vs_], [vtok], vs_)
        P.dma(P.pool, gp.t[t0:t0 + 512, :].rearrange("(c p) d -> p c d", p=128), gs_[:, :, :], [gs_], [gp], gs_)
    P.finish([qT, kT, ktok, vtok, xcT, gp])
    return P


def build_E(S):
    P = Prog()
    NCH = S // 128
    identf = small_in(P, "ident_f", [128, 128])
    tri = small_in(P, "tri", [128, 128])
    ones = small_in(P, "ones", [128, 128])
    maskT = small_in(P, "maskT", [128, 128])
    gq = small_in(P, "gq", [128, 4, 2, NCH])
    bg = small_in(P, "bg", [128, 2])
    skip = small_in(P, "skip", [128, 8])
    gng = bcast_vec(P, "gn_g", 1024)
    qT = P.dram("qT", [1024, S], BF16, "ExternalInput")
    kT = P.dram("kT", [1024, S], BF16, "ExternalInput")
    ktok = P.dram("ktok", [S, 1024], BF16, "ExternalInput")
    vtok = P.dram("vtok", [S, 1024], BF16, "ExternalInput")
    xcT = P.dram("xcT", [1024, S], BF16, "ExternalInput")
    zsT = P.dram("zsT", [1024, S], BF16, "ExternalInput")
    hsT = P.dram("hsT", [1024, S], BF16, "ExternalOutput")
    dv = P.dve
    bsm = P.ps("bsm", [128, 512], F32)
    bsT = P.ps("bsT", [128, 512], F32)
    bN = [P.ps("bN", [128, 512], F32) for _ in range(2)]
    bU = Ring([P.ps("bU", [128, 512], F32) for _ in range(2)])
    ptf = Ring([P.ps("ptf", [128, 4, 128], F32) for _ in range(2)])
    gs = P.sb("gs", [128, 2, NCH], F32)
    P.op(dv, lambda e: e.tensor_tensor(out=gs[:, :, :], in0=gq[:, 0, :, :], in1=gq[:, 1, :, :], op=ALU.add), [gq], [gs])
    P.op(dv, lambda e: e.tensor_tensor(out=gs[:, :, :], in0=gs[:, :, :], in1=gq[:, 2, :, :], op=ALU.add), [gq, gs], [gs])
    P.op(dv, lambda e: e.tensor_tensor(out=gs[:, :, :], in0=gs[:, :, :], in1=gq[:, 3, :, :], op=ALU.add), [gq, gs], [gs])
    for w_ in range(2):
        P.op(dv, lambda e, w_=w_: e.tensor_scalar(out=gs[:, w_, :], in0=gs[:, w_, :], scalar1=bg[:, w_:w_ + 1], scalar2=None,
                                                  op0=ALU.add, op1=ALU.bypass), [gs, bg], [gs])
    tb = {n: P.sb(n, [128, NCH], F32) for n in ("e1", "sp", "thr", "bvec", "wv", "decay", "t1", "t2")}
    P.op(P.act, lambda e: e.activation(out=tb["e1"][:, :], in_=gs[:, 1, :], func=AF.Exp, scale=-1.0), [gs], [tb["e1"]])
    P.op(P.act, lambda e: e.activation(out=tb["sp"][:, :], in_=tb["e1"][:, :], func=AF.Ln, bias=1.0), [tb["e1"]], [tb["sp"]])
    P.mm(bsm[:, 0:NCH], [(tri[:, :], tb["sp"][:, :])], [tri, tb["sp"]], [bsm])
    P.mm(bsT[:, 0:NCH], [(ones[:, :], tb["sp"][:, :])], [ones, tb["sp"]], [bsT])
    P.op(P.act, lambda e: e.activation(out=tb["thr"][:, :], in_=bsm[:, 0:NCH], func=AF.Exp), [bsm], [tb["thr"]])
    P.op(dv, lambda e: e.tensor_tensor(out=tb["t1"][:, :], in0=bsm[:, 0:NCH], in1=gs[:, 0, :], op=ALU.add), [bsm, gs], [tb["t1"]])
    P.op(P.act, lambda e: e.activation(out=tb["bvec"][:, :], in_=tb["t1"][:, :], func=AF.Exp), [tb["t1"]], [tb["bvec"]])
    P.op(dv, lambda e: e.tensor_tensor(out=tb["t2"][:, :], in0=tb["t1"][:, :], in1=bsT[:, 0:NCH], op=ALU.subtract), [tb["t1"], bsT], [tb["t2"]])
    P.op(P.act, lambda e: e.activation(out=tb["wv"][:, :], in_=tb["t2"][:, :], func=AF.Exp), [tb["t2"]], [tb["wv"]])
    P.op(P.act, lambda e: e.activation(out=tb["decay"][:, :], in_=bsT[:, 0:NCH], func=AF.Exp, scale=-1.0), [bsT], [tb["decay"]])
    Cs = P.sb("C", [128, 8, 1024], F32)
    Cb = P.sb("Cb", [128, 8, 1024], BF16)
    ns = P.sb("n", [128, 8], F32)
    nb = P.sb("nb", [128, 8], BF16)
    onesb = P.sb("onesb", [128, 1], BF16)
    P.op(P.pool, lambda e: e.memset(Cs[:, :, :], 0.0), [], [Cs])
    P.op(P.pool, lambda e: e.memset(Cb[:, :, :], 0.0), [], [Cb])
    P.op(P.pool, lambda e: e.memset(ns[:, :], 0.0), [], [ns])
    P.op(P.pool, lambda e: e.memset(nb[:, :], 0.0), [], [nb])
    P.op(P.pool, lambda e: e.memset(onesb[:, :], 1.0), [], [onesb])
    qr = Ring([P.sb("qt", [128, 8, 512], BF16, dma=True) for _ in range(2)])
    kr = Ring([P.sb("kt", [128, 8, 512], BF16, dma=True) for _ in range(2)])
    ktr = Ring([P.sb("ktk", [128, 4, 1024], BF16, dma=True) for _ in range(2)])
    vtr = Ring([P.sb("vtk", [128, 4, 1024], BF16, dma=True) for _ in range(2)])
    xr = Ring([P.sb("xct", [128, 8, 512], BF16, dma=True) for _ in range(2)])
    zr = Ring([P.sb("zst", [128, 8, 512], BF16, dma=True) for _ in range(2)])
    hr = Ring([P.sb("hst", [128, 8, 512], BF16, dma=True) for _ in range(2)])
    sm = P.sb("sm", [128, 128], BF16)
    kw = P.sb("kw", [128, 1024], BF16)
    hn = P.sb("hn", [128, 1024], F32)
    hg = P.sb("hg", [128, 1024], F32)
    tmpT = P.sb("tmpT", [128, 8, 128], F32)
    st12 = P.sb("st12", [128, 12], F32)
    mv = P.sb("mv", [128, 2], F32)
    sc = {n: P.sb(n, [128, 1], F32) for n in ("dmax", "rden", "r2", "vh", "rstd", "scl")}
    for grp in range(S // 512):
        t0 = grp * 512
        qt, kt, ktk, vtk, xct, zst, hst = qr.next(), kr.next(), ktr.next(), vtr.next(), xr.next(), zr.next(), hr.next()
        fm = lambda T_: T_.t[:, t0:t0 + 512].rearrange("(a p) t -> p a t", p=128)
        tm = lambda T_: T_.t[t0:t0 + 512, :].rearrange("(c p) d -> p c d", p=128)
        P.dma(P.sp, qt[:, :, :], fm(qT), [qT], [qt], qt)
        P.dma(P.sp, kt[:, :, :], fm(kT), [kT], [kt], kt)
        P.dma(P.sp, ktk[:, :, :], tm(ktok), [ktok], [ktk], ktk)
        P.dma(P.sp, vtk[:, :, :], tm(vtok), [vtok], [vtk], vtk)
        P.dma(P.sp, xct[:, :, :], fm(xcT), [xcT], [xct], xct)
        P.dma(P.sp, zst[:, :, :], fm(zsT), [zsT], [zst], zst)
        for c in range(4):
            ch = grp * 4 + c
            cs = slice(c * 128, (c + 1) * 128)
            col = lambda n: tb[n][:, ch:ch + 1]
            P.mm(bsT[:, 0:128], [(kt[:, a, cs], qt[:, a, cs]) for a in range(8)], [kt, qt], [bsT])
            P.op(dv, lambda e: e.scalar_tensor_tensor(out=sm[:, :], in0=bsT[:, 0:128], scalar=col("bvec"), in1=maskT[:, :],
                                                      op0=ALU.mult, op1=ALU.mult), [bsT, tb["bvec"], maskT], [sm])
            for hf in range(2):
                es = slice(hf * 512, (hf + 1) * 512)
                P.mm(bN[hf][:, :], [(sm[:, :], vtk[:, c, es])] + [(qt[:, a, cs], Cb[:, a, es]) for a in range(8)],
                     [sm, vtk, qt, Cb], [bN[hf]])
            P.mm(bsm[:, 0:1], [(sm[:, :], onesb[:, 0:1])] + [(qt[:, a, cs], nb[:, a:a + 1]) for a in range(8)],
                 [sm, onesb, qt, nb], [bsm])
            s_ = lambda n: sc[n][:, :]
            P.op(P.act, lambda e: e.activation(out=s_("dmax"), in_=bsm[:, 0:1], func=AF.Abs), [bsm], [sc["dmax"]])
            P.op(dv, lambda e: e.tensor_tensor(out=s_("dmax"), in0=s_("dmax"), in1=col("thr"), op=ALU.max), [sc["dmax"], tb["thr"]], [sc["dmax"]])
            P.op(dv, lambda e: e.reciprocal(out=s_("rden"), in_=s_("dmax")), [sc["dmax"]], [sc["rden"]])
            P.op(dv, lambda e: e.bn_stats(out=st12[:, 0:6], in_=bN[0][:, :]), [bN[0]], [st12])
            P.op(dv, lambda e: e.bn_stats(out=st12[:, 6:12], in_=bN[1][:, :]), [bN[1]], [st12])
            P.op(dv, lambda e: e.bn_aggr(out=mv[:, :], in_=st12[:, :]), [st12], [mv])
            P.op(dv, lambda e: e.tensor_tensor(out=s_("r2"), in0=s_("rden"), in1=s_("rden"), op=ALU.mult), [sc["rden"]], [sc["r2"]])
            P.op(dv, lambda e: e.tensor_scalar(out=s_("vh"), in0=mv[:, 1:2], scalar1=s_("r2"), scalar2=EPS, op0=ALU.mult, op1=ALU.add),
                 [mv, sc["r2"]], [sc["vh"]])
            P.op(P.act, lambda e: e.activation(out=s_("rstd"), in_=s_("vh"), func=AF.Sqrt), [sc["vh"]], [sc["rstd"]])
            P.op(dv, lambda e: e.reciprocal(out=s_("rstd"), in_=s_("rstd")), [sc["rstd"]], [sc["rstd"]])
            P.op(dv, lambda e: e.tensor_tensor(out=s_("scl"), in0=s_("rstd"), in1=s_("rden"), op=ALU.mult), [sc["rstd"], sc["rden"]], [sc["scl"]])
            for hf in range(2):
                es = slice(hf * 512, (hf + 1) * 512)
                P.op(dv, lambda e, hf=hf, es=es: e.tensor_scalar(out=hn[:, es], in0=bN[hf][:, :], scalar1=mv[:, 0:1], scalar2=s_("scl"),
                                                              op0=ALU.subtract, op1=ALU.mult), [bN[hf], mv, sc["scl"]], [hn])
            P.op(P.pool, lambda e: e.tensor_tensor(out=hg[:, :], in0=hn[:, :], in1=gng[:, :], op=ALU.mult), [hn, gng], [hg])
            for half in range(2):
                pt = ptf.next()
                for j in range(4):
                    ec = half * 4 + j
                    P.tr(pt[:, j, :], hg[:, ec * 128:(ec + 1) * 128], identf[:, :], [hg, identf], [pt])
                for j in range(4):
                    ec = half * 4 + j
                    P.op(dv, lambda e, j=j, ec=ec, pt=pt: e.scalar_tensor_tensor(out=tmpT[:, ec, :], in0=xct[:, ec, cs], scalar=skip[:, ec:ec + 1],
                                                                             in1=pt[:, j, :], op0=ALU.mult, op1=ALU.add),
                         [xct, skip, pt], [tmpT])
            P.op(P.pool, lambda e: e.tensor_tensor(out=hst[:, :, cs], in0=tmpT[:, :, :], in1=zst[:, :, cs], op=ALU.mult), [tmpT, zst], [hst])
            P.op(P.act, lambda e: e.activation(out=kw[:, :], in_=ktk[:, c, :], func=AF.Copy, scale=col("wv")), [ktk, tb["wv"]], [kw])
            for a in range(8):
                for hf in range(2):
                    es = slice(hf * 512, (hf + 1) * 512)
                    b = bU.next()
                    P.mm(b[:, :], [(kw[:, a * 128:(a + 1) * 128], vtk[:, c, es])], [kw, vtk], [b])
                    P.op(dv, lambda e, a=a, es=es, b=b: e.scalar_tensor_tensor(out=Cs[:, a, es], in0=Cs[:, a, es], scalar=col("decay"),
                                                                           in1=b[:, :], op0=ALU.mult, op1=ALU.add),
                         [Cs, tb["decay"], b], [Cs])
            for a in range(8):
                P.mm(bsm[:, 8 + a:9 + a], [(kw[:, a * 128:(a + 1) * 128], onesb[:, 0:1])], [kw, onesb], [bsm])
            P.op(dv, lambda e: e.scalar_tensor_tensor(out=ns[:, :], in0=ns[:, :], scalar=col("decay"), in1=bsm[:, 8:16],
                                                      op0=ALU.mult, op1=ALU.add), [ns, tb["decay"], bsm], [ns])
            P.op(P.act, lambda e: e.activation(out=Cb[:, 0:4, :], in_=Cs[:, 0:4, :], func=AF.Copy), [Cs], [Cb])
            P.op(P.pool, lambda e: e.tensor_copy(out=Cb[:, 4:8, :], in_=Cs[:, 4:8, :]), [Cs], [Cb])
            P.op(dv, lambda e: e.tensor_copy(out=nb[:, :], in_=ns[:, :]), [ns], [nb])
        P.dma(P.pool, hsT.t[:, t0:t0 + 512].rearrange("(a p) t -> p a t", p=128), hst[:, :, :], [hst], [hsT], hst)
    P.finish([hsT])
    return P


_PROGS = {}


def _prog(key, fn):
    if key not in _PROGS:
        _PROGS[key] = fn()
    return _PROGS[key]


def _run(P, maps):
    res = run_bass_kernel_spmd(P.nc, maps, core_ids=list(range(NCORE)))
    return res.results


def _flat(w):
    return np.ascontiguousarray(w, dtype=np.float32).reshape(-1, 512)


def _cat(parts, axis):
    return np.ascontiguousarray(np.concatenate(parts, axis=axis))


def bd_np(w, hh):
    M = np.zeros((128, 8, 128), np.float32)
    for kc in range(8):
        blocks = w[hh * 256 + kc * 32: hh * 256 + (kc + 1) * 32]
        for nl in range(32):
            M[4 * nl:4 * nl + 4, kc, 4 * nl:4 * nl + 4] = blocks[nl].T
    return M


def kernel(x, positions, norm_mix_g, norm_ffn_g, ret_w_in, ret_gn_g, ret_w_out,
           ml_w_in, ml_conv_w, ml_conv_b, ml_w_q, ml_w_k, ml_w_v, ml_w_gate, ml_b_gate,
           ml_gn_g, ml_skip, ml_w_out, ffn_w_gate, ffn_w_up, ffn_w_down, final_g):
    f32 = lambda a: np.asarray(a, dtype=np.float32)
    x = f32(x)
    positions = np.asarray(positions).astype(np.int32)
    B, S, _ = x.shape
    NSEG = NCORE // B
    TOK = S // NSEG
    ident = ident_np()
    invf = (10000.0 ** (-np.arange(128, dtype=np.float32) * np.float32(2.0 / 256))).astype(np.float32).reshape(128, 1)
    bs = lambda c: (c // NSEG, c % NSEG)
    seg = lambda s: slice(s * TOK, (s + 1) * TOK)

    PA = _prog(("A", TOK), lambda: build_A(TOK))
    wA = _flat(f32(ret_w_in)[0])
    mapsA = []
    for c in range(NCORE):
        b, s = bs(c)
        mapsA.append({"x": np.ascontiguousarray(x[b, seg(s)]), "pos": np.ascontiguousarray(positions[b, seg(s)]),
                      "g_mix": f32(norm_mix_g)[0], "gn_g": f32(ret_gn_g)[0], "invf": invf, "zeta": zeta_np(),
                      "ret_w_in": wA, "ident_bf": ident})
    rA = _run(PA, mapsA)
    PB = _prog(("B", S), lambda: build_B(S))
    mapsB = []
    for c in range(NCORE):
        b, hp = bs(c)
        h0 = 2 * hp
        dm, xi, dec = ret_consts(h0)
        cores = [b * NSEG + s for s in range(NSEG)]
        mapsB.append({"ident_bf": ident, "dm": dm, "xi": xi, "dec": dec,
                      "qT": _cat([rA[k]["QT"][h0 * 256:(h0 + 2) * 256] for k in cores], 1),
                      "kT": _cat([rA[k]["KT"][h0 * 256:(h0 + 2) * 256] for k in cores], 1),
                      "kz": _cat([rA[k]["KZ"][:, h0 * 256:(h0 + 2) * 256] for k in cores], 0),
                      "v": _cat([rA[k]["V"][:, h0 * 512:(h0 + 2) * 512] for k in cores], 0),
                      "g": _cat([rA[k]["G"][:, h0 * 512:(h0 + 2) * 512] for k in cores], 0)})
    del rA
    rB = _run(PB, mapsB)
    del mapsB
    PC = _prog(("C", TOK), lambda: build_CF(TOK, False))
    wts = {"w_out": _flat(f32(ret_w_out)[0]), "w_gate": _flat(f32(ffn_w_gate)[0]), "w_up": _flat(f32(ffn_w_up)[0]),
           "w_down": _flat(f32(ffn_w_down)[0]), "ml_w_in": _flat(f32(ml_w_in)[0])}
    mapsC = []
    for c in range(NCORE):
        b, s = bs(c)
        m = {"hin": np.ascontiguousarray(x[b, seg(s)]),
             "mixT": _cat([rB[b * NSEG + hp]["yT"][:, seg(s)] for hp in range(NSEG)], 0),
             "g_ffn": f32(norm_ffn_g)[0], "g_nxt": f32(norm_mix_g)[1], "ident_bf": ident}
        m.update(wts)
        mapsC.append(m)
    del rB
    rC = _run(PC, mapsC)
    del mapsC
    PD = _prog(("D", S), lambda: build_D(S))
    cwf, cbf = f32(ml_conv_w)[0], f32(ml_conv_b)[0]
    wgf = f32(ml_w_gate)[0]
    mapsD = []
    for c in range(NCORE):
        b, hh = bs(c)
        fs = slice(hh * 1024, (hh + 1) * 1024)
        mapsD.append({"xmT": _cat([rC[b * NSEG + s]["xmT"][fs] for s in range(NSEG)], 1),
                      "conv_w": np.ascontiguousarray(cwf[:, fs].reshape(4, 8, 128).transpose(2, 1, 0)),
                      "conv_b": np.ascontiguousarray(cbf[fs].reshape(8, 128).T),
                      "wq_bd": bd_np(f32(ml_w_q)[0], hh), "wk_bd": bd_np(f32(ml_w_k)[0], hh), "wv_bd": bd_np(f32(ml_w_v)[0], hh),
                      "wgate": np.ascontiguousarray(np.stack([wgf[wi * 4096 + hh * 1024: wi * 4096 + (hh + 1) * 1024].reshape(8, 128, 8)
                                                              for wi in range(3)], 0).transpose(2, 0, 1, 3))})
    rD = _run(PD, mapsD)
    del mapsD
    PE = _prog(("E", S), lambda: build_E(S))
    NCH = S // 128
    tri = np.triu(np.ones((128, 128), np.float32))
    bgf = f32(ml_b_gate)[0]
    mapsE = []
    for c in range(NCORE):
        b, hh = bs(c)
        fs = slice(hh * 1024, (hh + 1) * 1024)
        gq = np.stack([rD[b * NSEG + j]["gp"][:, [hh, 4 + hh]].reshape(NCH, 128, 2).transpose(1, 2, 0) for j in range(NSEG)], 1)
        mapsE.append({"ident_f": np.eye(128, dtype=np.float32), "tri": tri, "ones": np.ones((128, 128), np.float32), "maskT": tri,
                      "gq": np.ascontiguousarray(gq), "bg": np.ascontiguousarray(np.tile(bgf[[hh, 4 + hh]][None, :], (128, 1))),
                      "skip": np.ascontiguousarray(f32(ml_skip)[0][fs].reshape(8, 128).T), "gn_g": np.ascontiguousarray(f32(ml_gn_g)[0][fs]),
                      "qT": rD[c]["qT"], "kT": rD[c]["kT"], "ktok": rD[c]["ktok"], "vtok": rD[c]["vtok"], "xcT": rD[c]["xcT"],
                      "zsT": _cat([rC[b * NSEG + s]["zsT"][fs] for s in range(NSEG)], 1)})
    del rD
    rE = _run(PE, mapsE)
    del mapsE
    PF = _prog(("F", TOK), lambda: build_CF(TOK, True))
    wts = {"w_out": _flat(f32(ml_w_out)[0]), "w_gate": _flat(f32(ffn_w_gate)[1]), "w_up": _flat(f32(ffn_w_up)[1]),
           "w_down": _flat(f32(ffn_w_down)[1])}
    mapsF = []
    for c in range(NCORE):
        b, s = bs(c)
        m = {"hin": rC[c]["hout"], "mixT": _cat([rE[b * NSEG + hh]["hsT"][:, seg(s)] for hh in range(NSEG)], 0),
             "g_ffn": f32(norm_ffn_g)[1], "g_nxt": f32(final_g), "ident_bf": ident}
        m.update(wts)
        mapsF.append(m)
    del rE, rC
    rF = _run(PF, mapsF)
    out = np.zeros((B, S, D), np.float32)
    for c in range(NCORE):
        b, s = bs(c)
        out[b, seg(s)] = rF[c]["out"]
    return out
```
